# Optimizing a Trainium2 kernel written in Bass

```python
import math
import jax, jax.numpy as jnp
from jax import lax
import numpy as np

D_MODEL = 4096
BATCH = 4
SEQ = 4096
DEPTH = 2
DEC_BATCH = 16
DEC_SEQ = 32
PAST_LEN = 2048

CHUNK = 64
N_MIXERS = 2
N_A = (DEPTH + 1) // 2
N_B = DEPTH // 2
MLSTM_PF = 2
INNER = MLSTM_PF * D_MODEL
N_HEADS = 8
DK = INNER // N_HEADS
DV = INNER // N_HEADS
QKV_BLOCK = 4
N_QKV_BLOCKS = INNER // QKV_BLOCK
MCONV_W = 4
CONV_CH = D_MODEL
CONV_W = 31
ALPHA = (2 * DEPTH) ** 0.25
BETA = (8 * DEPTH) ** -0.25
LN_EPS = 1e-5

kernel_name = "mlstm_conformer_conv_stream_step"


def _layernorm(x, g, b=None):
    xf = x.astype(jnp.float32)
    mu = jnp.mean(xf, axis=-1, keepdims=True)
    var = jnp.mean(jnp.square(xf - mu), axis=-1, keepdims=True)
    y = (xf - mu) * lax.rsqrt(var + LN_EPS) * g.astype(jnp.float32)
    if b is not None:
        y = y + b.astype(jnp.float32)
    return y.astype(x.dtype)


def _causal_dwconv(x, hist, w, b):
    W = w.shape[0]
    T = x.shape[1]
    xp = jnp.concatenate([hist.astype(x.dtype), x], axis=1)
    wc = w.astype(x.dtype)
    y = sum(xp[:, j:j + T] * wc[j] for j in range(W))
    return y + b.astype(x.dtype), xp[:, xp.shape[1] - (W - 1):]


def _headwise(x, w):
    B, T, _ = x.shape
    xb = x.reshape(B, T, N_QKV_BLOCKS, QKV_BLOCK)
    return jnp.einsum('btnd,nde->btne', xb, w.astype(x.dtype)).reshape(B, T, INNER)


def _mlstm_cell(q, k, v, ig, lf, C0, n0, m0):
    B, H, T, _ = q.shape
    L = min(CHUNK, T)
    NC = T // L

    def to_chunks(a):
        return jnp.moveaxis(a.reshape(a.shape[:2] + (NC, L) + a.shape[3:]), 2, 0)

    causal = jnp.tril(jnp.ones((L, L), dtype=bool))

    def step(carry, inp):
        C, n, m = carry
        qc, kc, vc, ic, fc = inp
        b = jnp.cumsum(fc, axis=-1)
        logD = jnp.where(causal, b[..., :, None] - b[..., None, :] + ic[..., None, :], -jnp.inf)
        g = b + m[..., None]
        m_t = jnp.maximum(g, jnp.max(logD, axis=-1))
        Dm = jnp.exp(logD - m_t[..., None])
        inter = jnp.exp(g - m_t)
        S = jnp.einsum('bhtd,bhsd->bhts', qc, kc) * Dm
        num = jnp.einsum('bhts,bhsv->bhtv', S, vc) + inter[..., None] * jnp.einsum('bhtd,bhdv->bhtv', qc, C)
        den = jnp.sum(S, axis=-1) + inter * jnp.einsum('bhtd,bhd->bht', qc, n)
        h = num / jnp.maximum(jnp.abs(den), jnp.exp(-m_t))[..., None]
        bL = b[..., -1]
        logw = bL[..., None] - b + ic
        decay = bL + m
        m_new = jnp.maximum(decay, jnp.max(logw, axis=-1))
        wk = kc * jnp.exp(logw - m_new[..., None])[..., None]
        sc = jnp.exp(decay - m_new)
        C_new = sc[..., None, None] * C + jnp.einsum('bhsd,bhsv->bhdv', wk, vc)
        n_new = sc[..., None] * n + jnp.sum(wk, axis=2)
        return (C_new, n_new, m_new), h

    (C, n, m), hs = lax.scan(step, (C0, n0, m0), tuple(map(to_chunks, (q, k, v, ig, lf))))
    h = jnp.moveaxis(hs, 0, 2).reshape(B, H, T, DV)
    return h, C, n, m


def _mlstm_mixer(x, C0, n0, m0, hist, w_up, w_mconv, b_mconv, w_q, w_k, w_v, w_gate, b_gate, mh_gain, skip, w_down):
    B, T, _ = x.shape
    f32 = jnp.float32
    up = x @ w_up.astype(x.dtype)
    xm, z = jnp.split(up, 2, axis=-1)
    xc, hist_new = _causal_dwconv(xm, hist, w_mconv, b_mconv)
    xa = jax.nn.silu(xc)
    q = _headwise(xa, w_q)
    k = _headwise(xa, w_k)
    v = _headwise(xm, w_v)
    wg = w_gate.astype(x.dtype)
    gates = (q @ wg[0] + k @ wg[1] + v @ wg[2]).astype(f32) + b_gate.astype(f32)
    ig = jnp.transpose(gates[..., :N_HEADS], (0, 2, 1))
    lf = jnp.transpose(jax.nn.log_sigmoid(gates[..., N_HEADS:]), (0, 2, 1))

    def heads(a):
        return a.reshape(B, T, N_HEADS, -1).transpose(0, 2, 1, 3).astype(f32)

    h, C, n, m = _mlstm_cell(heads(q), heads(k) * (DK ** -0.5), heads(v), ig, lf,
                             C0.astype(f32), n0.astype(f32), m0.astype(f32))
    hn = _layernorm(h, mh_gain.reshape(N_HEADS, 1, DV))
    hn = hn.transpose(0, 2, 1, 3).reshape(B, T, INNER).astype(x.dtype)
    out = ((hn + skip.astype(x.dtype) * xa) * jax.nn.silu(z)) @ w_down.astype(x.dtype)
    return out, C, n, m, hist_new


def _conv_mixer(x, hist, w_cin, b_cin, w_dw, b_dw, cln_g, cln_b, w_cout, b_cout):
    proj = x @ w_cin.astype(x.dtype) + b_cin.astype(x.dtype)
    a, gl, zg = jnp.split(proj, 3, axis=-1)
    u = a * jax.nn.sigmoid(gl)
    c, hist_new = _causal_dwconv(u, hist, w_dw, b_dw)
    c = jax.nn.silu(_layernorm(c, cln_g, cln_b))
    out = (c * jax.nn.silu(zg)) @ w_cout.astype(x.dtype) + b_cout.astype(x.dtype)
    return out, hist_new


def _trunk(x, st_C, st_n, st_m, st_mconv, st_cconv,
           w_up, w_mconv, b_mconv, w_q, w_k, w_v, w_gate, b_gate, mh_gain, skip, w_down,
           w_cin, b_cin, w_dw, b_dw, cln_g, cln_b, w_cout, b_cout, post_ln_g, post_ln_b):
    Cs, ns, ms, mcs, ccs = [], [], [], [], []
    for i in range(DEPTH):
        j = i // N_MIXERS
        if i % N_MIXERS == 0:
            out, C, n, m, hc = _mlstm_mixer(x, st_C[j], st_n[j], st_m[j], st_mconv[j],
                                            w_up[j], w_mconv[j], b_mconv[j], w_q[j], w_k[j], w_v[j],
                                            w_gate[j], b_gate[j], mh_gain[j], skip[j], w_down[j])
            Cs.append(C); ns.append(n); ms.append(m); mcs.append(hc)
        else:
            out, hc = _conv_mixer(x, st_cconv[j], w_cin[j], b_cin[j], w_dw[j], b_dw[j],
                                  cln_g[j], cln_b[j], w_cout[j], b_cout[j])
            ccs.append(hc)
        x = _layernorm(ALPHA * x + out, post_ln_g[i], post_ln_b[i])
    dt = x.dtype
    return (x, jnp.stack(Cs).astype(dt), jnp.stack(ns).astype(dt), jnp.stack(ms).astype(dt),
            jnp.stack(mcs).astype(dt), jnp.stack(ccs).astype(dt))


def setup_inputs(seed: int = 0) -> dict:
    key = jax.random.key(seed)
    ks = jax.random.split(key, 32)
    nrm = jax.random.normal
    f = jnp.float32
    b_gate = jnp.concatenate([
        0.1 * nrm(ks[10], (N_A, N_HEADS), f),
        jnp.broadcast_to(jnp.linspace(3.0, 6.0, N_HEADS, dtype=f), (N_A, N_HEADS)) + 0.01 * nrm(ks[11], (N_A, N_HEADS), f),
    ], axis=-1)
    return {
        "x_prompt": nrm(ks[0], (BATCH, SEQ, D_MODEL), f),
        "x_sample": nrm(ks[1], (DEC_BATCH, DEC_SEQ, D_MODEL), f),
        "state_mlstm_C": 0.02 * nrm(ks[2], (N_A, DEC_BATCH, N_HEADS, DK, DV), f),
        "state_mlstm_n": 0.02 * nrm(ks[3], (N_A, DEC_BATCH, N_HEADS, DK), f),
        "state_mlstm_m": nrm(ks[4], (N_A, DEC_BATCH, N_HEADS), f),
        "state_mlstm_conv": nrm(ks[5], (N_A, DEC_BATCH, MCONV_W - 1, INNER), f),
        "state_conformer_conv": 0.5 * nrm(ks[6], (N_B, DEC_BATCH, CONV_W - 1, CONV_CH), f),
        "w_up": nrm(ks[7], (N_A, D_MODEL, 2 * INNER), f) * D_MODEL ** -0.5,
        "w_mconv": nrm(ks[8], (N_A, MCONV_W, INNER), f) * MCONV_W ** -0.5,
        "b_mconv": 0.01 * nrm(ks[9], (N_A, INNER), f),
        "w_q": nrm(ks[12], (N_A, N_QKV_BLOCKS, QKV_BLOCK, QKV_BLOCK), f) * QKV_BLOCK ** -0.5,
        "w_k": nrm(ks[13], (N_A, N_QKV_BLOCKS, QKV_BLOCK, QKV_BLOCK), f) * QKV_BLOCK ** -0.5,
        "w_v": nrm(ks[14], (N_A, N_QKV_BLOCKS, QKV_BLOCK, QKV_BLOCK), f) * QKV_BLOCK ** -0.5,
        "w_gate": nrm(ks[15], (N_A, 3, INNER, 2 * N_HEADS), f) * (0.1 * (3 * INNER) ** -0.5),
        "b_gate": b_gate,
        "mh_gain": 1.0 + 0.01 * nrm(ks[16], (N_A, INNER), f),
        "skip": 1.0 + 0.01 * nrm(ks[17], (N_A, INNER), f),
        "w_down": nrm(ks[18], (N_A, INNER, D_MODEL), f) * (INNER ** -0.5 * BETA),
        "w_cin": nrm(ks[19], (N_B, D_MODEL, 3 * CONV_CH), f) * D_MODEL ** -0.5,
        "b_cin": 0.01 * nrm(ks[20], (N_B, 3 * CONV_CH), f),
        "w_dw": nrm(ks[21], (N_B, CONV_W, CONV_CH), f) * CONV_W ** -0.5,
        "b_dw": 0.01 * nrm(ks[22], (N_B, CONV_CH), f),
        "cln_g": 1.0 + 0.01 * nrm(ks[23], (N_B, CONV_CH), f),
        "cln_b": 0.01 * nrm(ks[24], (N_B, CONV_CH), f),
        "w_cout": nrm(ks[25], (N_B, CONV_CH, D_MODEL), f) * (CONV_CH ** -0.5 * BETA),
        "b_cout": 0.01 * nrm(ks[26], (N_B, D_MODEL), f),
        "post_ln_g": 1.0 + 0.01 * nrm(ks[27], (DEPTH, D_MODEL), f),
        "post_ln_b": 0.01 * nrm(ks[28], (DEPTH, D_MODEL), f),
    }


def reference(x_prompt, x_sample, state_mlstm_C, state_mlstm_n, state_mlstm_m, state_mlstm_conv, state_conformer_conv,
              w_up, w_mconv, b_mconv, w_q, w_k, w_v, w_gate, b_gate, mh_gain, skip, w_down,
              w_cin, b_cin, w_dw, b_dw, cln_g, cln_b, w_cout, b_cout, post_ln_g, post_ln_b):
    params = (w_up, w_mconv, b_mconv, w_q, w_k, w_v, w_gate, b_gate, mh_gain, skip, w_down,
              w_cin, b_cin, w_dw, b_dw, cln_g, cln_b, w_cout, b_cout, post_ln_g, post_ln_b)
    B = x_prompt.shape[0]
    f32 = jnp.float32
    z_C = jnp.zeros((N_A, B, N_HEADS, DK, DV), f32)
    z_n = jnp.zeros((N_A, B, N_HEADS, DK), f32)
    z_m = jnp.zeros((N_A, B, N_HEADS), f32)
    z_mc = jnp.zeros((N_A, B, MCONV_W - 1, INNER), x_prompt.dtype)
    z_cc = jnp.zeros((N_B, B, CONV_W - 1, CONV_CH), x_prompt.dtype)
    y_prompt, p_C, p_n, p_m, p_mc, p_cc = _trunk(x_prompt, z_C, z_n, z_m, z_mc, z_cc, *params)
    y_sample, s_C, s_n, s_m, s_mc, s_cc = _trunk(x_sample, state_mlstm_C, state_mlstm_n, state_mlstm_m,
                                                 state_mlstm_conv, state_conformer_conv, *params)
    return (y_prompt, y_sample, p_C, p_n, p_m, p_mc, p_cc, s_C, s_n, s_m, s_mc, s_cc)
```

```python
import math
from contextlib import ExitStack
import numpy as np
import concourse.bass as bass
import concourse.mybir as mybir
from concourse.bass_utils import run_bass_kernel_spmd

F32 = mybir.dt.float32
BF16 = mybir.dt.bfloat16
AF = mybir.ActivationFunctionType
ALU = mybir.AluOpType

D = 4096
INNER = 8192
NH = 8
DK = 1024
TP = 4096
NSS = 2
SL = 32
TS = NSS * SL
TT = TP + TS
PRE = 1920
OWN0 = 2048
ALPHA = 4.0 ** 0.25
EPS = 1e-5
NCORES = 8


class Buf:
    __slots__ = ("w", "r")

    def __init__(self):
        self.w = None
        self.r = {}


class Sem:
    __slots__ = ("h", "val", "key", "is_dma")

    def __init__(self, h, key, is_dma):
        self.h = h
        self.val = 0
        self.key = key
        self.is_dma = is_dma


class Eng:
    def __init__(self, fw, name, eng, self_sync=True):
        self.fw = fw
        self.eng = eng
        self.sem = fw.new_sem("e_" + name, False)
        self.seen = {}
        self.self_sync = self_sync

    def wait_tok(self, tok):
        if tok is None:
            return
        sem, val = tok
        if sem is self.sem and not self.self_sync:
            return
        if sem.is_dma:
            val = sem.val
        if self.seen.get(sem.key, 0) >= val:
            return
        self.eng.wait_ge(sem.h, val)
        self.seen[sem.key] = val

    def deps(self, reads, writes):
        for b in reads:
            self.wait_tok(b.w)
        for b in writes:
            self.wait_tok(b.w)
            for t in list(b.r.values()):
                self.wait_tok(t)

    def commit(self, ins, reads, writes, inc=True):
        if inc:
            ins.then_inc(self.sem.h, 1)
            self.sem.val += 1
            tok = (self.sem, self.sem.val)
        else:
            tok = (self.sem, self.sem.val + 1)
        for b in reads:
            b.r[tok[0].key] = tok
        for b in writes:
            b.w = tok
            b.r = {}
        return tok

    def op(self, fn, reads, writes, inc=True):
        self.deps(reads, writes)
        ins = fn(self.eng)
        return self.commit(ins, reads, writes, inc)


class FW:
    def __init__(self, nc, stack):
        self.nc = nc
        self.stack = stack
        self.nsem = 0
        self.dsems = []
        self.allsems = []
        self.pe = Eng(self, "pe", nc.tensor, self_sync=False)
        self.dve = Eng(self, "dve", nc.vector)
        self.act = Eng(self, "act", nc.scalar)
        self.pool = Eng(self, "pool", nc.gpsimd)
        self.sp = Eng(self, "sp", nc.sync)
        self.engs = [self.pe, self.dve, self.act, self.pool, self.sp]
        self.out_toks = []

    def new_sem(self, name, is_dma=True, in_barrier=True):
        h = self.stack.enter_context(self.nc.semaphore(name))
        self.nsem += 1
        s = Sem(h, name + "_" + str(self.nsem), is_dma)
        if is_dma:
            self.allsems.append(s)
            if in_barrier:
                self.dsems.append(s)
        return s

    def dma(self, out, in_, reads, writes, dsem, q=None, is_output=False, slow=False):
        q = q or self.sp
        q.deps(reads, writes)
        if slow:
            ins = q.eng.dma_start(out=out, in_=in_, allow_slow_non_contiguous=True)
        else:
            ins = q.eng.dma_start(out=out, in_=in_)
        ins.then_inc(dsem.h, 16)
        dsem.val += 16
        tok = (dsem, dsem.val)
        for b in reads:
            b.r[tok[0].key] = tok
        for b in writes:
            b.w = tok
            b.r = {}
        if is_output:
            self.out_toks.append(tok)
        return tok

    def barrier(self):
        for e in self.engs:
            for o in self.engs:
                if o is not e and o.sem.val > 0:
                    e.wait_tok((o.sem, o.sem.val))
            for s in self.dsems:
                if s.val > 0:
                    e.wait_tok((s, s.val))

    def finish(self):
        for s in self.allsems:
            if s.val > 0:
                self.sp.wait_tok((s, s.val))


def build_program():
    nc = bass.Bass("TRN2", target_bir_lowering=False)

    def din(name, shape, dt=F32):
        return nc.dram_tensor(name, list(shape), dt, kind="ExternalInput").ap()

    def dout(name, shape, dt=F32):
        return nc.dram_tensor(name, list(shape), dt, kind="ExternalOutput").ap()

    def dscr(name, shape, dt):
        return nc.dram_tensor(name, list(shape), dt, kind="Internal").ap()

    xp = din("xp", [TP, D])
    xs = din("xs", [TS, D])
    sC = din("sC", [NSS, NH, DK, DK])
    sn = din("sn", [NSS, NH, DK])
    sm = din("sm", [NSS, NH])
    smc = din("smc", [NSS, 3, INNER])
    scc = din("scc", [NSS, 30, D])
    w_up = din("w_up", [D, 2 * INNER])
    w_down = din("w_down", [INNER, D])
    w_cin = din("w_cin", [D, 3 * D])
    w_cout = din("w_cout", [D, D])
    wmc = din("wmc", [128, 64, 4])
    bmc = din("bmc", [128, 64])
    skipc = din("skipc", [128, 64])
    gainc = din("gainc", [128, 64])
    wq4 = din("wq4", [128, 64, 4])
    wk4 = din("wk4", [128, 64, 4])
    wv4 = din("wv4", [128, 64, 4])
    wg40 = din("wg40", [128, 3 * 64 * 40])
    bg40 = din("bg40", [40, 1])
    bcin = din("bcin", [128, 96])
    wdw = din("wdw", [128, 32 * 31])
    bdw = din("bdw", [128, 32])
    clng = din("clng", [128, 32])
    clnb = din("clnb", [128, 32])
    bcout = din("bcout", [1, D])
    plg = din("plg", [2, D])
    plb = din("plb", [2, D])
    identd = din("ident", [128, 128])
    mask01d = din("mask01", [128, 128])
    bdmaskd = din("bdmask", [128, 32])
    flagd = din("flag", [128, 1])
    yp = dout("yp", [TP - OWN0, D])
    ys = dout("ys", [TS, D])
    pC = dout("pC", [NH, DK, DK])
    pn = dout("pn", [NH, DK])
    pm = dout("pm", [NH, 1])
    pmc = dout("pmc", [3, INNER])
    pcc = dout("pcc", [30, D])
    oC = dout("oC", [NSS, NH, DK, DK])
    on = dout("on", [NSS, NH, DK])
    om = dout("om", [NSS, NH, 1])
    omc = dout("omc", [NSS, 3, INNER])
    occ = dout("occ", [NSS, 30, D])
    WupB = dscr("WupB", [64, 128, 32, 256], BF16)
    WcinB = dscr("WcinB", [48, 128, 32, 256], BF16)
    WdB = dscr("WdB", [8, 8, 128, 8, 512], BF16)
    WcoB = dscr("WcoB", [8, 4, 128, 8, 512], BF16)
    DgD = dscr("DgD", [32, 128, 31, 128], BF16)
    QT = dscr("QT", [INNER, TT], BF16)
    KT = dscr("KT", [INNER, TT], BF16)
    SXA = dscr("SXA", [INNER, TT], BF16)
    SZ = dscr("SZ", [INNER, TT], BF16)
    Kt = dscr("Kt", [TT, INNER], BF16)
    Vt = dscr("Vt", [TT, INNER], BF16)
    GATES = dscr("GATES", [16, TT], F32)
    X1 = dscr("X1", [TT, D], F32)
    SZG = dscr("SZG", [D, TT], BF16)

    tiles = []
    for (t0_, n_, light_) in [(0, 512, True), (512, 512, True), (1024, 512, True), (1536, 384, True),
                              (1920, 384, False), (2304, 384, False), (2688, 512, False), (3200, 512, False), (3712, 384, False)]:
        tiles.append(dict(t0=t0_, ntok=n_, segs=[(0, n_)], kind="p", light=light_, eblk=(t0_ == PRE)))
    tiles.append(dict(t0=TP, ntok=TS, segs=[(i * SL, SL) for i in range(NSS)], kind="s", light=False, eblk=False))
    ftiles = [t for t in tiles if not t["light"]]
    for k_, t_ in enumerate(ftiles):
        t_["fidx"] = k_
    GTt = [dscr(f"GTt{k_}", [128, 64, t_["ntok"]], BF16) for k_, t_ in enumerate(ftiles)]
    PTt = [dscr(f"PTt{k_}", [128, 32, t_["ntok"]], BF16) for k_, t_ in enumerate(ftiles)]

    with ExitStack() as gs:
        fw = FW(nc, gs)
        pe, dve, act, pool, sp = fw.pe, fw.dve, fw.act, fw.pool, fw.sp

        uid = [0]

        def T(st, name, shape, dt):
            uid[0] += 1
            return st.enter_context(nc.sbuf_tensor(f"{name}_{uid[0]}", list(shape), dt))

        def mm(out, lhsT, rhs, start, stop, reads, writes, inc=False):
            return pe.op(lambda e: e.matmul(out, lhsT=lhsT, rhs=rhs, start=start, stop=stop), reads, writes, inc=inc)

        def tr(out, in_, ident, reads, writes, inc=True):
            return pe.op(lambda e: e.transpose(out=out, in_=in_, identity=ident), reads, writes, inc=inc)

        def actf(out, in_, func, reads, writes, bias=None, scale=None):
            kw = {}
            if bias is not None:
                kw["bias"] = bias
            if scale is not None:
                kw["scale"] = scale
            return act.op(lambda e: e.activation(out=out, in_=in_, func=func, **kw), reads, writes)

        banks = [gs.enter_context(nc.psum_tensor(f"bank{i}", [128, 512], F32)) for i in range(8)]
        bb = [Buf() for _ in range(8)]

        ident = T(gs, "ident", [128, 128], F32)
        identb = T(gs, "identb", [128, 128], BF16)
        mask01 = T(gs, "mask01", [128, 128], F32)
        onesb = T(gs, "onesb", [128, 128], BF16)
        histP = T(gs, "histP", [128, 64, 3], F32)
        histS = T(gs, "histS", [128, 64, 3 * NSS], F32)
        chP32 = T(gs, "chP32", [128, 32, 30], F32)
        chS32 = T(gs, "chS32", [128, 32, 30 * NSS], F32)
        b_const = Buf()
        b_histP, b_histS, b_chP, b_chS = Buf(), Buf(), Buf(), Buf()
        ds_c = fw.new_sem("ds_c")
        ds_out = fw.new_sem("ds_out")
        ds_st = [fw.new_sem(f"ds_st{i}") for i in range(4)]
        fw.dma(ident[:], identd[:, :], [], [b_const], ds_c)
        fw.dma(mask01[:], mask01d[:, :], [], [b_const], ds_c)
        flag = T(gs, "flag", [128, 1], F32)
        fw.dma(flag[:], flagd[:, :], [], [b_const], ds_c)
        dve.op(lambda e: e.tensor_copy(out=identb[:], in_=ident[:]), [b_const], [b_const])
        dve.op(lambda e: e.memset(onesb[:], 1.0), [], [b_const])
        dve.op(lambda e: e.memset(histP[:], 0.0), [], [b_histP])
        dve.op(lambda e: e.memset(chP32[:], 0.0), [], [b_chP])

        b_Wup = [Buf() for _ in range(4)]
        b_Wd, b_Wcin, b_Wco, b_Dg = Buf(), Buf(), Buf(), Buf()
        dwu = [fw.new_sem(f"dwu{i}", True, False) for i in range(4)]
        dw = [fw.new_sem(f"dw{i}", True, False) for i in range(4)]
        for u in range(64):
            fw.dma(WupB[u], w_up[:, u * 256:(u + 1) * 256].rearrange("(kc p) c -> p kc c", p=128), [], [b_Wup[u // 16]], dwu[u // 16], q=pool)
        deferred_casts = []

        def _mk(dst, src, buf, sem):
            return lambda: fw.dma(dst, src, [], [buf], sem, q=pool)

        for og in range(8):
            for fg in range(8):
                deferred_casts.append(_mk(WdB[og, fg], w_down[fg * 1024:(fg + 1) * 1024, og * 512:(og + 1) * 512].rearrange("(fl p) c -> p fl c", p=128),
                                          b_Wd, dw[1]))
        for u in range(48):
            deferred_casts.append(_mk(WcinB[u], w_cin[:, u * 256:(u + 1) * 256].rearrange("(kc p) c -> p kc c", p=128), b_Wcin, dw[2]))
        for og in range(8):
            for fg in range(4):
                deferred_casts.append(_mk(WcoB[og, fg], w_cout[fg * 1024:(fg + 1) * 1024, og * 512:(og + 1) * 512].rearrange("(fl p) c -> p fl c", p=128),
                                          b_Wco, dw[3]))
        deferred_casts.reverse()

        with ExitStack() as st:
            stg = T(st, "stg", [32, INNER], F32)
            b_stg = Buf()
            wdws = T(st, "wdws", [128, 32 * 31], F32)
            dgs = [T(st, f"dgs{i}", [128, 31, 128], BF16) for i in range(2)]
            b_dgs = [Buf(), Buf()]
            fw.dma(wdws[:], wdw[:, :], [], [b_const], ds_c)
            for i in range(NSS):
                fw.dma(stg[0:3, :], smc[i], [], [b_stg], ds_c)
                for c in range(64):
                    bk = c % 2
                    tr(banks[bk][:, 0:3], stg[0:3, c * 128:(c + 1) * 128], ident[0:3, 0:3], [b_stg, b_const], [bb[bk]])
                    act.op(lambda e: e.copy(out=histS[:, c, i * 3:(i + 1) * 3], in_=banks[bk][:, 0:3]), [bb[bk]], [b_histS])
                fw.dma(stg[0:30, 0:D], scc[i], [], [b_stg], ds_c)
                for c in range(32):
                    bk = c % 2
                    tr(banks[bk][:, 0:30], stg[0:30, c * 128:(c + 1) * 128], ident[0:30, 0:30], [b_stg, b_const], [bb[bk]])
                    act.op(lambda e: e.copy(out=chS32[:, c, i * 30:(i + 1) * 30], in_=banks[bk][:, 0:30]), [bb[bk]], [b_chS])
            for c in range(32):
                k = c % 2
                for j in range(31):
                    eng = act if j % 2 == 0 else dve
                    if eng is act:
                        actf(dgs[k][:, j, :], identb[:], AF.Identity, [b_const], [b_dgs[k]], scale=wdws[:, c * 31 + j:c * 31 + j + 1])
                    else:
                        dve.op(lambda e: e.tensor_scalar(out=dgs[k][:, j, :], in0=identb[:], scalar1=wdws[:, c * 31 + j:c * 31 + j + 1],
                                                         scalar2=None, op0=ALU.mult), [b_const], [b_dgs[k]])
                fw.dma(DgD[c], dgs[k][:], [b_dgs[k]], [b_Dg], ds_st[k])
            fw.barrier()

        def load_xT(st_bufs, tile, src):
            xT, b_xT, xin, b_xin, ds_x, pbanks = st_bufs
            ntok = tile["ntok"]
            nsub = (ntok + 127) // 128
            for sub in range(nsub):
                ns = min(128, ntok - sub * 128)
                fw.dma(xin[0:ns, :], src[sub * 128:sub * 128 + ns, :], [], [b_xin], ds_x)
                for g in range(8):
                    bk = pbanks[g % 2]
                    for j in range(4):
                        c = g * 4 + j
                        tr(banks[bk][:, j * 128:j * 128 + ns], xin[0:ns, c * 128:(c + 1) * 128], ident[0:ns, 0:ns],
                           [b_xin, b_const], [bb[bk]], inc=(j == 3))
                    src_ap = banks[bk][:, :].rearrange("p (j t) -> p j t", t=128)[:, :, 0:ns]
                    dst_ap = xT[:, g * 4:(g + 1) * 4, sub * 128:sub * 128 + ns]
                    if g % 2 == 0:
                        act.op(lambda e: e.copy(out=dst_ap, in_=src_ap), [bb[bk]], [b_xT])
                    else:
                        dve.op(lambda e: e.tensor_copy(out=dst_ap, in_=src_ap), [bb[bk]], [b_xT])

        with ExitStack() as st:
            WB = [T(st, f"WB{i}", [128, 64, 128], BF16) for i in range(3)]
            wgb = T(st, "wgb", [128, 3 * 64 * 40], BF16)
            wmcs = T(st, "wmcs", [128, 64, 4], F32)
            bmcs = T(st, "bmcs", [128, 64], F32)
            skips = T(st, "skips", [128, 64], F32)
            bgs = T(st, "bgs", [40, 1], F32)
            nbgs = T(st, "nbgs", [40, 1], F32)
            gsb = T(st, "gsb", [40, 512], F32)
            b_par = Buf()
            b_gsb = Buf()
            with ExitStack() as st2:
                w4s = T(st2, "w4s", [128, 3, 64, 4], F32)
                bdm = T(st2, "bdm", [128, 32], F32)
                wgs = T(st2, "wgs", [128, 3 * 64 * 40], F32)
                fw.dma(w4s[:, 0], wq4[:, :, :], [], [b_par], ds_c)
                fw.dma(w4s[:, 1], wk4[:, :, :], [], [b_par], ds_c)
                fw.dma(w4s[:, 2], wv4[:, :, :], [], [b_par], ds_c)
                fw.dma(bdm[:], bdmaskd[:, :], [], [b_par], ds_c)
                fw.dma(wgs[:], wg40[:, :], [], [b_par], ds_c)
                fw.dma(wmcs[:], wmc[:, :, :], [], [b_par], ds_c)
                fw.dma(bmcs[:], bmc[:, :], [], [b_par], ds_c)
                fw.dma(skips[:], skipc[:, :], [], [b_par], ds_c)
                fw.dma(bgs[:], bg40[:, :], [], [b_par], ds_c)
                dve.op(lambda e: e.tensor_copy(out=wgb[:], in_=wgs[:]), [b_par], [b_par])
                dve.op(lambda e: e.tensor_scalar(out=nbgs[:], in0=bgs[:], scalar1=-1.0, scalar2=None, op0=ALU.mult), [b_par], [b_par])
                for j in range(3):
                    for c in range(64):
                        o = WB[j][:, c, :].rearrange("p (n e) -> p n e", e=4)
                        i0 = w4s[:, j, c, :].unsqueeze(1).to_broadcast([128, 32, 4])
                        i1 = bdm[:].unsqueeze(2).to_broadcast([128, 32, 4])
                        dve.op(lambda e: e.tensor_tensor(out=o, in0=i0, in1=i1, op=ALU.mult), [b_par], [b_par])
                fw.barrier()
            xT = T(st, "xT", [128, 32, 512], BF16)
            b_xT = Buf()
            xin = T(st, "xin", [128, D], F32)
            b_xin = Buf()
            Wsl = [T(st, f"Wsl{i}", [128, 32, 256], BF16) for i in range(3)]
            b_Wsl = [Buf() for _ in range(3)]
            ds_W = [fw.new_sem(f"ds_W{i}") for i in range(3)]
            ds_x = fw.new_sem("ds_x")
            xm32 = [T(st, f"xm32{i}", [128, 520], F32) for i in range(2)]
            xmb = [T(st, f"xmb{i}", [128, 512], BF16) for i in range(2)]
            acc = [T(st, f"acc{i}", [128, 512], F32) for i in range(2)]
            xa = [T(st, f"xa{i}", [128, 512], BF16) for i in range(2)]
            sxa = [T(st, f"sxa{i}", [128, 512], BF16) for i in range(2)]
            qTs = [T(st, f"qTs{i}", [128, 512], BF16) for i in range(2)]
            kTs = [T(st, f"kTs{i}", [128, 512], BF16) for i in range(2)]
            vTs = [T(st, f"vTs{i}", [128, 512], BF16) for i in range(2)]
            kts = [T(st, f"kts{i}", [128, 4, 128], BF16) for i in range(2)]
            vts = [T(st, f"vts{i}", [128, 4, 128], BF16) for i in range(2)]
            szs = [T(st, f"szs{i}", [128, 512], BF16) for i in range(2)]
            b_xm, b_xmb, b_acc, b_xa, b_sxa = ([Buf(), Buf()] for _ in range(5))
            b_q, b_k, b_v, b_kt, b_vt, b_sz = ([Buf(), Buf()] for _ in range(6))
            ds_s = [fw.new_sem(f"ds_sA{i}") for i in range(2)]
            wgv = wgb[:].rearrange("p (j c m) -> p j c m", j=3, c=64)
            BU = [0, 1]
            BQ, BK, BV, BKT, BVT, BG = 2, 3, 4, 5, 6, 7

            for tile in tiles:
                t0, ntok, segs = tile["t0"], tile["ntok"], tile["segs"]
                nsub = (ntok + 127) // 128
                is_s = tile["kind"] == "s"
                hist, b_hist = (histS, b_histS) if is_s else (histP, b_histP)
                src = xs if is_s else xp[t0:t0 + ntok, :]
                load_xT((xT, b_xT, xin, b_xin, ds_x, (BQ, BK)), tile, src)
                for u in range(2):
                    fw.dma(Wsl[u % 3][:], WupB[u], [b_Wup[u // 16]], [b_Wsl[u % 3]], ds_W[u % 3])
                light = tile["light"]
                ncg = 64 if light else 128

                def stage_U(cg):
                    u, cc = cg // 2, cg % 2
                    if cg % 3 == 0 and deferred_casts:
                        deferred_casts.pop()()
                    if cc == 0 and u + 2 < ncg // 2:
                        s2 = (u + 2) % 3
                        fw.dma(Wsl[s2][:], WupB[u + 2], [b_Wup[(u + 2) // 16]], [b_Wsl[s2]], ds_W[s2])
                    sl = u % 3
                    bk = BU[cg % 2]
                    for kc in range(32):
                        mm(banks[bk][:, 0:ntok], Wsl[sl][:, kc, cc * 128:(cc + 1) * 128], xT[:, kc, 0:ntok],
                           kc == 0, kc == 31, [b_Wsl[sl], b_xT], [bb[bk]], inc=(kc == 31))

                def stage_z(cg):
                    par = cg % 2
                    bk = BU[cg % 2]
                    c = cg - 64
                    actf(szs[par][:, 0:ntok], banks[bk][:, 0:ntok], AF.Silu, [bb[bk]], [b_sz[par]])
                    fw.dma(SZ[c * 128:(c + 1) * 128, t0:t0 + ntok], szs[par][:, 0:ntok], [b_sz[par]], [], ds_s[par])

                def stage_chain(c):
                    par = c % 2
                    bk = BU[c % 2]
                    for si, (so, sln) in enumerate(segs):
                        eo = si * (sln + 3)
                        dve.op(lambda e: e.tensor_copy(out=xm32[par][:, eo:eo + 3], in_=hist[:, c, si * 3:si * 3 + 3]),
                               [b_hist], [b_xm[par]])
                        act.op(lambda e: e.copy(out=xm32[par][:, eo + 3:eo + 3 + sln], in_=banks[bk][:, so:so + sln]),
                               [bb[bk]], [b_xm[par]])
                        dve.op(lambda e: e.tensor_copy(out=hist[:, c, si * 3:si * 3 + 3], in_=xm32[par][:, eo + sln:eo + sln + 3]),
                               [b_xm[par]], [b_hist])
                        act.op(lambda e: e.copy(out=xmb[par][:, so:so + sln], in_=xm32[par][:, eo + 3:eo + 3 + sln]),
                               [b_xm[par]], [b_xmb[par]])
                        dve.op(lambda e: e.tensor_scalar(out=acc[par][:, so:so + sln], in0=xm32[par][:, eo:eo + sln],
                                                         scalar1=wmcs[:, c, 0:1], scalar2=bmcs[:, c:c + 1], op0=ALU.mult, op1=ALU.add),
                               [b_xm[par], b_par], [b_acc[par]])
                        for j in range(1, 4):
                            dve.op(lambda e: e.scalar_tensor_tensor(out=acc[par][:, so:so + sln], in0=xm32[par][:, eo + j:eo + j + sln],
                                                                    scalar=wmcs[:, c, j:j + 1], in1=acc[par][:, so:so + sln],
                                                                    op0=ALU.mult, op1=ALU.add),
                                   [b_xm[par], b_acc[par], b_par], [b_acc[par]])
                    actf(xa[par][:, 0:ntok], acc[par][:, 0:ntok], AF.Silu, [b_acc[par]], [b_xa[par]])
                    if not light:
                        actf(sxa[par][:, 0:ntok], xa[par][:, 0:ntok], AF.Identity, [b_xa[par], b_par], [b_sxa[par]], scale=skips[:, c:c + 1])
                        fw.dma(SXA[c * 128:(c + 1) * 128, t0:t0 + ntok], sxa[par][:, 0:ntok], [b_sxa[par]], [], ds_s[par])

                def stage_Q(c):
                    par = c % 2
                    mm(banks[BQ][:, 0:ntok], WB[0][:, c, :], xa[par][:, 0:ntok], True, True, [b_par, b_xa[par]], [bb[BQ]], inc=True)
                    mm(banks[BK][:, 0:ntok], WB[1][:, c, :], xa[par][:, 0:ntok], True, True, [b_par, b_xa[par]], [bb[BK]], inc=True)
                    mm(banks[BV][:, 0:ntok], WB[2][:, c, :], xmb[par][:, 0:ntok], True, True, [b_par, b_xmb[par]], [bb[BV]], inc=True)
                    for sub in range(nsub):
                        ns = min(128, ntok - sub * 128)
                        mm(banks[BKT][0:ns, sub * 128:(sub + 1) * 128], xa[par][:, sub * 128:sub * 128 + ns], WB[1][:, c, :],
                           True, True, [b_par, b_xa[par]], [bb[BKT]], inc=(sub == nsub - 1))
                    for sub in range(nsub):
                        ns = min(128, ntok - sub * 128)
                        mm(banks[BVT][0:ns, sub * 128:(sub + 1) * 128], xmb[par][:, sub * 128:sub * 128 + ns], WB[2][:, c, :],
                           True, True, [b_par, b_xmb[par]], [bb[BVT]], inc=(sub == nsub - 1))
                    act.op(lambda e: e.copy(out=qTs[par][:, 0:ntok], in_=banks[BQ][:, 0:ntok]), [bb[BQ]], [b_q[par]])
                    dve.op(lambda e: e.tensor_copy(out=kTs[par][:, 0:ntok], in_=banks[BK][:, 0:ntok]), [bb[BK]], [b_k[par]])
                    act.op(lambda e: e.copy(out=vTs[par][:, 0:ntok], in_=banks[BV][:, 0:ntok]), [bb[BV]], [b_v[par]])
                    nsl = min(128, ntok)
                    dve.op(lambda e: e.tensor_copy(out=kts[par][0:nsl, 0:nsub, :],
                                                   in_=banks[BKT][0:nsl, 0:nsub * 128].rearrange("p (s f) -> p s f", f=128)),
                           [bb[BKT]], [b_kt[par]])
                    act.op(lambda e: e.copy(out=vts[par][0:nsl, 0:nsub, :],
                                            in_=banks[BVT][0:nsl, 0:nsub * 128].rearrange("p (s f) -> p s f", f=128)),
                           [bb[BVT]], [b_vt[par]])

                def stage_G(c):
                    par = c % 2
                    for j, (srcb, bsrc) in enumerate(((qTs, b_q), (kTs, b_k), (vTs, b_v))):
                        mm(banks[BG][0:40, 0:ntok], wgv[:, j, c, :], srcb[par][:, 0:ntok], (c == 0 and j == 0), (c == 63 and j == 2),
                           [b_par, bsrc[par]], [bb[BG]], inc=(j == 2))
                    if not light:
                        fw.dma(QT[c * 128:(c + 1) * 128, t0:t0 + ntok], qTs[par][:, 0:ntok], [b_q[par]], [], ds_s[par])
                        fw.dma(KT[c * 128:(c + 1) * 128, t0:t0 + ntok], kTs[par][:, 0:ntok], [b_k[par]], [], ds_s[par])
                    if ntok % 128 == 0:
                        fw.dma(Kt[t0:t0 + ntok, c * 128:(c + 1) * 128].rearrange("(s p) f -> p s f", p=128), kts[par][:, 0:nsub, :],
                               [b_kt[par]], [], ds_s[par])
                        fw.dma(Vt[t0:t0 + ntok, c * 128:(c + 1) * 128].rearrange("(s p) f -> p s f", p=128), vts[par][:, 0:nsub, :],
                               [b_vt[par]], [], ds_s[par])
                    else:
                        fw.dma(Kt[t0:t0 + ntok, c * 128:(c + 1) * 128], kts[par][0:ntok, 0, :], [b_kt[par]], [], ds_s[par])
                        fw.dma(Vt[t0:t0 + ntok, c * 128:(c + 1) * 128], vts[par][0:ntok, 0, :], [b_vt[par]], [], ds_s[par])
                    if c == 63:
                        actf(gsb[0:8, 0:ntok], banks[BG][0:8, 0:ntok], AF.Identity, [bb[BG], b_par], [b_gsb], bias=bgs[0:8, :])
                        actf(gsb[32:40, 0:ntok], banks[BG][32:40, 0:ntok], AF.Exp, [bb[BG], b_par], [b_gsb], bias=nbgs[32:40, :], scale=-1.0)
                        actf(gsb[32:40, 0:ntok], gsb[32:40, 0:ntok], AF.Ln, [b_gsb], [b_gsb], bias=1.0)
                        dve.op(lambda e: e.tensor_scalar(out=gsb[32:40, 0:ntok], in0=gsb[32:40, 0:ntok], scalar1=-1.0, scalar2=None,
                                                         op0=ALU.mult), [b_gsb], [b_gsb])
                        fw.dma(GATES[0:8, t0:t0 + ntok], gsb[0:8, 0:ntok], [b_gsb], [], ds_s[0])
                        fw.dma(GATES[8:16, t0:t0 + ntok], gsb[32:40, 0:ntok], [b_gsb], [], ds_s[0])

                for cg in range(max(ncg, 66)):
                    if cg < ncg:
                        stage_U(cg)
                        if cg < 64:
                            stage_chain(cg)
                        else:
                            stage_z(cg)
                    if 1 <= cg <= 64:
                        stage_Q(cg - 1)
                    if 2 <= cg <= 65:
                        stage_G(cg - 2)
            while deferred_casts:
                deferred_casts.pop()()
            fw.barrier()

        with ExitStack() as st:
            stg = T(st, "stgo", [3, INNER], F32)
            b_stg = Buf()
            for (hsrc, bh, off, dst) in [(histP, b_histP, 0, pmc)] + [(histS, b_histS, 3 * i, omc[i]) for i in range(NSS)]:
                for c in range(64):
                    bk = c % 2
                    tr(banks[bk][0:3, 0:128], hsrc[:, c, off:off + 3], ident[:, :], [bh, b_const], [bb[bk]])
                    act.op(lambda e: e.copy(out=stg[:, c * 128:(c + 1) * 128], in_=banks[bk][0:3, 0:128]), [bb[bk]], [b_stg])
                fw.dma(dst, stg[:, :], [b_stg], [], ds_out, is_output=True)
            fw.barrier()

        seqs = [dict(kind="p", T=TP, L=128, tok0=0, idx=0)] + [dict(kind="s", T=SL, L=SL, tok0=TP + i * SL, idx=i) for i in range(NSS)]
        for sq in seqs:
            Tq, L, tok0 = sq["T"], sq["L"], sq["tok0"]
            NCH = Tq // L
            is_s = sq["kind"] == "s"
            with ExitStack() as st:
                cols = T(st, "cols", [128, 512], F32)
                scb = T(st, "scb", [128, 32, 8], F32)
                b_cols, b_scb = Buf(), Buf()
                with ExitStack() as st2:
                    igr = T(st2, "igr", [8, Tq], F32)
                    lfr = T(st2, "lfr", [8, Tq], F32)
                    mtr = T(st2, "mtr", [8, Tq], F32)
                    br = T(st2, "br", [8, Tq], F32)
                    tmp = T(st2, "tmpr", [8, Tq], F32)
                    m0 = T(st2, "m0", [8, 1], F32)
                    McR = T(st2, "McR", [8, 32], F32)
                    mpR = T(st2, "mpR", [8, 32], F32)
                    scR = T(st2, "scR", [8, 32], F32)
                    selh = T(st2, "selh", [8, 8, 128], F32)
                    b_g = Buf()
                    fw.dma(igr[:], GATES[0:8, tok0:tok0 + Tq], [], [b_g], ds_c)
                    fw.dma(lfr[:], GATES[8:16, tok0:tok0 + Tq], [], [b_g], ds_c)
                    if is_s:
                        fw.dma(m0[:], sm[sq["idx"]].rearrange("(h o) -> h o", o=1), [], [b_g], ds_c, slow=True)
                    else:
                        dve.op(lambda e: e.memset(m0[:], 0.0), [], [b_g])
                    if is_s:
                        dve.op(lambda e: e.tensor_tensor_scan(out=mtr[:], data0=lfr[:], data1=igr[:], initial=m0[:, 0:1], op0=ALU.add, op1=ALU.max),
                               [b_g], [b_g])
                    else:
                        mmid = T(st2, "mmid", [8, 1], F32)
                        dve.op(lambda e: e.tensor_tensor_scan(out=mtr[:, 0:OWN0], data0=lfr[:, 0:OWN0], data1=igr[:, 0:OWN0], initial=m0[:, 0:1],
                                                              op0=ALU.add, op1=ALU.max), [b_g], [b_g])
                        dve.op(lambda e: e.tensor_tensor(out=mmid[:], in0=mtr[:, OWN0 - 1:OWN0], in1=flag[0:8, :], op=ALU.mult), [b_g, b_const], [b_g])
                        dve.op(lambda e: e.tensor_tensor_scan(out=mtr[:, OWN0:Tq], data0=lfr[:, OWN0:Tq], data1=igr[:, OWN0:Tq], initial=mmid[:, 0:1],
                                                              op0=ALU.add, op1=ALU.max), [b_g], [b_g])
                    dve.op(lambda e: e.memset(tmp[:], 1.0), [], [b_g])
                    dve.op(lambda e: e.memset(tmp[:].rearrange("p (c l) -> p c l", l=L)[:, :, 0:1], 0.0), [b_g], [b_g])
                    dve.op(lambda e: e.tensor_tensor_scan(out=br[:], data0=tmp[:], data1=lfr[:], initial=0.0, op0=ALU.mult, op1=ALU.add),
                           [b_g], [b_g])
                    mt3 = mtr[:].rearrange("p (c l) -> p c l", l=L)
                    b3 = br[:].rearrange("p (c l) -> p c l", l=L)
                    dve.op(lambda e: e.tensor_tensor(out=McR[:, 0:NCH], in0=mt3[:, :, L - 1], in1=b3[:, :, L - 1], op=ALU.subtract), [b_g], [b_g])
                    dve.op(lambda e: e.tensor_copy(out=mpR[:, 0:1], in_=m0[:, 0:1]), [b_g], [b_g])
                    if NCH > 1:
                        dve.op(lambda e: e.tensor_copy(out=mpR[:, 1:NCH], in_=mt3[:, 0:NCH - 1, L - 1]), [b_g], [b_g])
                    if not is_s:
                        cc0 = OWN0 // L
                        dve.op(lambda e: e.tensor_copy(out=mpR[:, cc0:cc0 + 1], in_=mmid[:, 0:1]), [b_g], [b_g])
                    dve.op(lambda e: e.tensor_tensor(out=scR[:, 0:NCH], in0=mpR[:, 0:NCH], in1=McR[:, 0:NCH], op=ALU.subtract), [b_g], [b_g])
                    actf(scR[:, 0:NCH], scR[:, 0:NCH], AF.Exp, [b_g], [b_g])
                    if not is_s:
                        dve.op(lambda e: e.tensor_tensor(out=scR[:, cc0:cc0 + 1], in0=scR[:, cc0:cc0 + 1], in1=flag[0:8, :], op=ALU.mult),
                               [b_g, b_const], [b_g])
                    mdst = om[sq["idx"]] if is_s else pm
                    fw.dma(mdst, mtr[:, Tq - 1:Tq], [b_g], [], ds_out, is_output=True, slow=True)
                    McB = McR[:, 0:NCH].unsqueeze(2).to_broadcast([8, NCH, L])
                    ig3 = igr[:].rearrange("p (c l) -> p c l", l=L)
                    lf3 = lfr[:].rearrange("p (c l) -> p c l", l=L)
                    dve.op(lambda e: e.tensor_tensor(out=ig3, in0=ig3, in1=b3, op=ALU.subtract), [b_g], [b_g])
                    dve.op(lambda e: e.tensor_tensor(out=ig3, in0=ig3, in1=McB, op=ALU.subtract), [b_g], [b_g])
                    actf(igr[:], igr[:], AF.Exp, [b_g], [b_g])
                    dve.op(lambda e: e.tensor_scalar(out=igr[:], in0=igr[:], scalar1=float(DK ** -0.5), scalar2=None, op0=ALU.mult), [b_g], [b_g])
                    dve.op(lambda e: e.tensor_tensor(out=lf3, in0=b3, in1=McB, op=ALU.add), [b_g], [b_g])
                    actf(lfr[:], lfr[:], AF.Exp, [b_g], [b_g], scale=-1.0)
                    BT = 0
                    for c in range(NCH):
                        for k, rr in enumerate((igr, lfr)):
                            o = (c * 2 + k) * 8
                            tr(banks[BT][0:L, o:o + 8], rr[:, c * L:(c + 1) * L], ident[0:8, 0:8], [b_g, b_const], [bb[BT]],
                               inc=(c == NCH - 1 and k == 1))
                    act.op(lambda e: e.copy(out=cols[0:L, 0:NCH * 16], in_=banks[BT][0:L, 0:NCH * 16]), [bb[BT]], [b_cols])
                    for h in range(NH):
                        dve.op(lambda e: e.tensor_copy(out=selh[:, h, :], in_=ident[0:8, h:h + 1].to_broadcast([8, 128])), [b_const], [b_g])
                    for h in range(NH):
                        mm(banks[1][:, h * 32:h * 32 + NCH], selh[:, h, :], scR[:, 0:NCH], True, True, [b_g], [bb[1]], inc=(h == NH - 1))
                    act.op(lambda e: e.copy(out=scb[:, 0:NCH, :].rearrange("p c h -> p h c"),
                                            in_=banks[1][:, 0:256].rearrange("p (h c) -> p h c", c=32)[:, :, 0:NCH]), [bb[1]], [b_scb])
                    fw.barrier()

                NCB = 2 if is_s else 1
                Csts = [T(st, f"Cst{i}", [128, 8, 1024], F32) for i in range(NCB)]
                nsts = [T(st, f"nst{i}", [128, 8], F32) for i in range(NCB)]
                b_Cs = [[Buf() for _ in range(8)] for _ in range(NCB)]
                b_ns = [Buf() for _ in range(NCB)]
                Cb = T(st, "Cb", [128, 8, 1024], BF16)
                nb = T(st, "nb", [128, 8], BF16)
                b_Cb = [Buf() for _ in range(8)]
                b_nb = Buf()
                NT = 512 if not is_s else SL
                NSUB = NT // L
                if is_s:
                    btiles = [(0, 1, False)]
                else:
                    btiles = [(0, 4, True), (4, 4, True), (8, 4, True), (12, 3, True)]
                    btiles += [(t_["t0"] // 128, t_["ntok"] // 128, False) for t_ in ftiles if t_["kind"] == "p"]
                    NLT = 4
                chunk_tile = {}
                for bi, (c0_, n_, lt_) in enumerate(btiles):
                    for k_ in range(n_):
                        chunk_tile[c0_ + k_] = (bi, k_)
                qTl = [T(st, f"qTl{i}", [128, 8, NT], BF16) for i in range(2)]
                kTl = [T(st, f"kTl{i}", [128, 8, NT], BF16) for i in range(2)]
                ktl = [T(st, f"ktl{i}", [128, NSUB, 1024], BF16) for i in range(2)]
                vtl = [T(st, f"vtl{i}", [128, NSUB, 1024], BF16) for i in range(2)]
                sxl = [T(st, f"sxl{i}", [128, 8, NT], BF16) for i in range(2)]
                szl = [T(st, f"szl{i}", [128, 8, NT], BF16) for i in range(2)]
                gTo = [T(st, f"gTo{i}", [128, 8, NT], BF16) for i in range(2)]
                b_ld = [[Buf() for _ in range(6)] for _ in range(2)]
                b_gTo = [Buf(), Buf()]
                ds_l = [[fw.new_sem(f"ds_l{sq['kind']}{sq['idx']}_{i}_{j}") for j in range(6)] for i in range(2)]
                wk = [T(st, f"wk{i}", [128, 1024], BF16) for i in range(2)]
                SDT = [T(st, f"SDT{i}", [128, 128], BF16) for i in range(2)]
                hraw = [T(st, f"hraw{i}", [128, 1024], F32) for i in range(2)]
                dens = [T(st, f"dens{i}", [128, 1], F32) for i in range(2)]
                hn = [T(st, f"hn{i}", [128, 1024], F32) for i in range(2)]
                gtmp = T(st, "gtmp", [128, 8, 128], F32)
                rec = T(st, "rec", [128, 1], F32)
                stt = T(st, "stt", [128, 12], F32)
                mv = T(st, "mv", [128, 2], F32)
                rstd = T(st, "rstd", [128, 1], F32)
                nmr = T(st, "nmr", [128, 1], F32)
                gains = T(st, "gains", [128, 64], F32)
                b_wk, b_SDT, b_hraw, b_dens, b_hn = ([Buf(), Buf()] for _ in range(5))
                b_gtmp, b_rec, b_stt, b_mv, b_rstd, b_nmr, b_gn = (Buf() for _ in range(7))
                fw.dma(gains[:], gainc[:, :], [], [b_gn], ds_c)
                BS, BN0, BN1, BD, BT0, BT1, BC0, BC1 = range(8)
                ntiles = len(btiles)
                nchunks = Tq // L
                lt = [0]

                def issue_loads(h, ti):
                    pp = lt[0] % 2
                    lt[0] += 1
                    c0_, n_, lt_ = btiles[ti]
                    ta = tok0 + c0_ * L
                    nt = n_ * L
                    rows = slice(h * 1024, (h + 1) * 1024)
                    fw.dma(ktl[pp][0:L, 0:n_, :], Kt[ta:ta + nt, rows].rearrange("(s p) f -> p s f", p=L), [], [b_ld[pp][2]], ds_l[pp][2])
                    fw.dma(vtl[pp][0:L, 0:n_, :], Vt[ta:ta + nt, rows].rearrange("(s p) f -> p s f", p=L), [], [b_ld[pp][3]], ds_l[pp][3])
                    if not lt_:
                        fw.dma(qTl[pp][:, :, 0:nt], QT[rows, ta:ta + nt].rearrange("(dc p) t -> p dc t", p=128), [], [b_ld[pp][0]], ds_l[pp][0])
                        fw.dma(kTl[pp][:, :, 0:nt], KT[rows, ta:ta + nt].rearrange("(dc p) t -> p dc t", p=128), [], [b_ld[pp][1]], ds_l[pp][1])
                        fw.dma(sxl[pp][:, :, 0:nt], SXA[rows, ta:ta + nt].rearrange("(dc p) t -> p dc t", p=128), [], [b_ld[pp][4]], ds_l[pp][4])
                        fw.dma(szl[pp][:, :, 0:nt], SZ[rows, ta:ta + nt].rearrange("(dc p) t -> p dc t", p=128), [], [b_ld[pp][5]], ds_l[pp][5])
                    return pp

                ds_Cl = [fw.new_sem(f"ds_Cl{sq['kind']}{sq['idx']}_{i}") for i in range(NCB)]

                def load_state(hh):
                    kk = hh % NCB
                    fw.dma(Csts[kk][:], sC[sq["idx"], hh].rearrange("(dc p) v -> p dc v", p=128), [], b_Cs[kk], ds_Cl[kk])
                    fw.dma(nsts[kk][:], sn[sq["idx"], hh].rearrange("(dc p) -> p dc", p=128), [], [b_ns[kk]], ds_Cl[kk], slow=True)

                for h in range(NH):
                    Cst, nst, b_C, b_n = Csts[h % NCB], nsts[h % NCB], b_Cs[h % NCB], b_ns[h % NCB]
                    if is_s:
                        if h == 0:
                            load_state(0)
                        if h + 1 < NH:
                            load_state(h + 1)
                    else:
                        pool.op(lambda e: e.memset(Cst[:], 0.0), [], b_C)
                        dve.op(lambda e: e.memset(nst[:], 0.0), [], [b_n])
                    tile_pp = {}
                    tile_pp[0] = issue_loads(h, 0)

                    def cinfo(i):
                        ti, sub = chunk_tile[i]
                        pp = tile_pp[ti]
                        tsl = slice(sub * L, (sub + 1) * L)
                        wexp = cols[0:L, (i * 2) * 8 + h:(i * 2) * 8 + h + 1]
                        flo = cols[0:L, (i * 2 + 1) * 8 + h:(i * 2 + 1) * 8 + h + 1]
                        scc_ = scb[:, i, h:h + 1]
                        return ti, sub, pp, tsl, wexp, flo, scc_

                    def stage_P(i):
                        ti, sub, pp, tsl, wexp, flo, scc_ = cinfo(i)
                        q = i % 2
                        for dc in range(8):
                            mm(banks[BS][0:L, 0:L], kTl[pp][:, dc, tsl], qTl[pp][:, dc, tsl], dc == 0, dc == 7,
                               [b_ld[pp][0], b_ld[pp][1]], [bb[BS]], inc=(dc == 7))
                        dve.op(lambda e: e.scalar_tensor_tensor(out=SDT[q][0:L, 0:L], in0=banks[BS][0:L, 0:L], scalar=wexp, in1=mask01[0:L, 0:L],
                                                                op0=ALU.mult, op1=ALU.mult), [bb[BS], b_cols, b_const], [b_SDT[q]])
                        actf(wk[q][0:L, :], ktl[pp][0:L, sub, :], AF.Identity, [b_ld[pp][2], b_cols], [b_wk[q]], scale=wexp)
                        for dc in range(8):
                            actf(Cb[:, dc, :], Cst[:, dc, :], AF.Identity, [b_C[dc], b_scb], [b_Cb[dc]], scale=scc_)
                        dve.op(lambda e: e.tensor_scalar(out=nb[:], in0=nst[:], scalar1=scc_, scalar2=None, op0=ALU.mult), [b_n, b_scb], [b_nb])

                    def stage_R(i):
                        ti, sub, pp, tsl, wexp, flo, scc_ = cinfo(i)
                        q = i % 2
                        for half, bn in enumerate((BN0, BN1)):
                            hs = slice(half * 512, (half + 1) * 512)
                            mm(banks[bn][0:L, :], SDT[q][0:L, 0:L], vtl[pp][0:L, sub, hs], True, False, [b_SDT[q], b_ld[pp][3]], [bb[bn]])
                            for dc in range(8):
                                mm(banks[bn][0:L, :], qTl[pp][:, dc, tsl], Cb[:, dc, hs], False, dc == 7, [b_ld[pp][0], b_Cb[dc]], [bb[bn]],
                                   inc=(dc == 7))
                        mm(banks[BD][0:L, 0:1], SDT[q][0:L, 0:L], onesb[0:L, 0:1], True, False, [b_SDT[q], b_const], [bb[BD]])
                        for dc in range(8):
                            mm(banks[BD][0:L, 0:1], qTl[pp][:, dc, tsl], nb[:, dc:dc + 1], False, dc == 7, [b_ld[pp][0], b_nb], [bb[BD]],
                               inc=(dc == 7))
                        for half, bn in enumerate((BN0, BN1)):
                            hs = slice(half * 512, (half + 1) * 512)
                            act.op(lambda e: e.copy(out=hraw[q][0:L, hs], in_=banks[bn][0:L, :]), [bb[bn]], [b_hraw[q]])
                        actf(dens[q][0:L, :], banks[BD][0:L, 0:1], AF.Abs, [bb[BD]], [b_dens[q]])
                        for dc in range(8):
                            for half in range(2):
                                bc = BC0 if half == 0 else BC1
                                hs = slice(half * 512, (half + 1) * 512)
                                mm(banks[bc][:, :], wk[q][0:L, dc * 128:(dc + 1) * 128], vtl[pp][0:L, sub, hs], True, True,
                                   [b_wk[q], b_ld[pp][3]], [bb[bc]], inc=True)
                                dve.op(lambda e: e.scalar_tensor_tensor(out=Cst[:, dc, hs], in0=Cst[:, dc, hs], scalar=scc_, in1=banks[bc][:, :],
                                                                        op0=ALU.mult, op1=ALU.add), [b_C[dc], b_scb, bb[bc]], [b_C[dc]])
                        for dc in range(8):
                            mm(banks[BS][:, 256 + dc:257 + dc], wk[q][0:L, dc * 128:(dc + 1) * 128], onesb[0:L, 0:1], True, True,
                               [b_wk[q], b_const], [bb[BS]], inc=(dc == 7))
                        dve.op(lambda e: e.scalar_tensor_tensor(out=nst[:], in0=nst[:], scalar=scc_, in1=banks[BS][:, 256:264],
                                                                op0=ALU.mult, op1=ALU.add), [b_n, b_scb, bb[BS]], [b_n])

                    def stage_H1(i):
                        ti, sub, pp, tsl, wexp, flo, scc_ = cinfo(i)
                        q = i % 2
                        dve.op(lambda e: e.tensor_scalar(out=rec[0:L, :], in0=dens[q][0:L, :], scalar1=flo, scalar2=None, op0=ALU.max),
                               [b_dens[q], b_cols], [b_rec])
                        dve.op(lambda e: e.reciprocal(out=rec[0:L, :], in_=rec[0:L, :]), [b_rec], [b_rec])
                        for half in range(2):
                            dve.op(lambda e: e.bn_stats(out=stt[0:L, half * 6:(half + 1) * 6], in_=hraw[q][0:L, half * 512:(half + 1) * 512]),
                                   [b_hraw[q]], [b_stt])
                        dve.op(lambda e: e.bn_aggr(out=mv[0:L, :], in_=stt[0:L, :]), [b_stt], [b_mv])
                        dve.op(lambda e: e.scalar_tensor_tensor(out=rstd[0:L, :], in0=mv[0:L, 1:2], scalar=rec[0:L, :], in1=rec[0:L, :],
                                                                op0=ALU.mult, op1=ALU.mult), [b_mv, b_rec], [b_rstd])
                        actf(rstd[0:L, :], rstd[0:L, :], AF.Sqrt, [b_rstd], [b_rstd], bias=EPS)
                        dve.op(lambda e: e.reciprocal(out=rstd[0:L, :], in_=rstd[0:L, :]), [b_rstd], [b_rstd])
                        dve.op(lambda e: e.tensor_tensor(out=rstd[0:L, :], in0=rstd[0:L, :], in1=rec[0:L, :], op=ALU.mult), [b_rstd, b_rec], [b_rstd])
                        dve.op(lambda e: e.scalar_tensor_tensor(out=nmr[0:L, :], in0=mv[0:L, 0:1], scalar=-1.0, in1=rstd[0:L, :],
                                                                op0=ALU.mult, op1=ALU.mult), [b_mv, b_rstd], [b_nmr])
                        actf(hn[q][0:L, :], hraw[q][0:L, :], AF.Identity, [b_hraw[q], b_rstd, b_nmr], [b_hn[q]], bias=nmr[0:L, :], scale=rstd[0:L, :])

                    def stage_H2(i):
                        ti, sub, pp, tsl, wexp, flo, scc_ = cinfo(i)
                        q = i % 2
                        for j in range(8):
                            bt = BT0 if j < 4 else BT1
                            jj = j % 4
                            tr(banks[bt][:, jj * 128:jj * 128 + L], hn[q][0:L, j * 128:(j + 1) * 128], ident[0:L, 0:L], [b_hn[q], b_const], [bb[bt]],
                               inc=(jj == 3))
                        for j in range(8):
                            bt = BT0 if j < 4 else BT1
                            jj = j % 4
                            actf(gtmp[:, j, 0:L], banks[bt][:, jj * 128:jj * 128 + L], AF.Identity, [bb[bt], b_gn], [b_gtmp],
                                 scale=gains[:, h * 8 + j:h * 8 + j + 1])
                        pool.op(lambda e: e.tensor_tensor(out=gtmp[:, :, 0:L], in0=gtmp[:, :, 0:L], in1=sxl[pp][:, :, tsl], op=ALU.add),
                                [b_gtmp, b_ld[pp][4]], [b_gtmp])
                        pool.op(lambda e: e.tensor_tensor(out=gTo[pp][:, :, tsl], in0=gtmp[:, :, 0:L], in1=szl[pp][:, :, tsl], op=ALU.mult),
                                [b_gtmp, b_ld[pp][5]], [b_gTo[pp]])
                        c0_, n_, lt_ = btiles[ti]
                        if sub == n_ - 1:
                            ta = tok0 + c0_ * L
                            nt = n_ * L
                            rows = slice(h * 1024, (h + 1) * 1024)
                            if is_s:
                                gdst = GTt[len(ftiles) - 1][:, h * 8:(h + 1) * 8, sq["idx"] * SL:(sq["idx"] + 1) * SL]
                            else:
                                gdst = GTt[ti - NLT][:, h * 8:(h + 1) * 8, :]
                            fw.dma(gdst, gTo[pp][:, :, 0:nt], [b_gTo[pp]], [], ds_st[pp])

                    def is_light(i):
                        return btiles[chunk_tile[i][0]][2]

                    def stage_P_light(i):
                        ti, sub, pp, tsl, wexp, flo, scc_ = cinfo(i)
                        q = i % 2
                        actf(wk[q][0:L, :], ktl[pp][0:L, sub, :], AF.Identity, [b_ld[pp][2], b_cols], [b_wk[q]], scale=wexp)

                    def stage_R_light(i):
                        ti, sub, pp, tsl, wexp, flo, scc_ = cinfo(i)
                        q = i % 2
                        for dc in range(8):
                            for half in range(2):
                                bc = BC0 if half == 0 else BC1
                                hs = slice(half * 512, (half + 1) * 512)
                                mm(banks[bc][:, :], wk[q][0:L, dc * 128:(dc + 1) * 128], vtl[pp][0:L, sub, hs], True, True,
                                   [b_wk[q], b_ld[pp][3]], [bb[bc]], inc=True)
                                dve.op(lambda e: e.scalar_tensor_tensor(out=Cst[:, dc, hs], in0=Cst[:, dc, hs], scalar=scc_, in1=banks[bc][:, :],
                                                                        op0=ALU.mult, op1=ALU.add), [b_C[dc], b_scb, bb[bc]], [b_C[dc]])
                        for dc in range(8):
                            mm(banks[BS][:, 256 + dc:257 + dc], wk[q][0:L, dc * 128:(dc + 1) * 128], onesb[0:L, 0:1], True, True,
                               [b_wk[q], b_const], [bb[BS]], inc=(dc == 7))
                        dve.op(lambda e: e.scalar_tensor_tensor(out=nst[:], in0=nst[:], scalar=scc_, in1=banks[BS][:, 256:264],
                                                                op0=ALU.mult, op1=ALU.add), [b_n, b_scb, bb[BS]], [b_n])

                    (stage_P_light if is_light(0) else stage_P)(0)
                    for i in range(nchunks):
                        (stage_R_light if is_light(i) else stage_R)(i)
                        if i + 1 < nchunks:
                            (stage_P_light if is_light(i + 1) else stage_P)(i + 1)
                        if not is_light(i):
                            stage_H1(i)
                        if i >= 1 and not is_light(i - 1):
                            stage_H2(i - 1)
                        ti_, sub_ = chunk_tile[i]
                        if sub_ == 0 and ti_ + 1 < ntiles:
                            tile_pp[ti_ + 1] = issue_loads(h, ti_ + 1)
                    if not is_light(nchunks - 1):
                        stage_H2(nchunks - 1)
                    Cdst = oC[sq["idx"], h] if is_s else pC[h]
                    ndst = on[sq["idx"], h] if is_s else pn[h]
                    fw.dma(Cdst.rearrange("(dc p) v -> p dc v", p=128), Cst[:], b_C, [], ds_out, is_output=True)
                    fw.dma(ndst.rearrange("(dc p) -> p dc", p=128), nst[:], [b_n], [], ds_out, is_output=True, slow=True)
                fw.barrier()

        def proj_res_ln(name, srcT, KC, Wscr, b_Wscr, nfg, resid, li, bias_bc_src, dst_of_tile, store_sem, NW):
            with ExitStack() as st:
                aT = T(st, name + "aT", [128, KC, 512], BF16)
                r = T(st, name + "r", [128, 4, D], F32)
                Ws = [T(st, name + f"Ws{i}", [128, 8, 512], BF16) for i in range(NW)]
                xres = T(st, name + "xres", [128, 4, 512], F32)
                gbc = T(st, name + "gbc", [128, D], F32)
                bbc = T(st, name + "bbc", [128, D], F32)
                stt = T(st, name + "stt", [128, 48], F32)
                mv = T(st, name + "mv", [128, 2], F32)
                rstd = T(st, name + "rstd", [128, 1], F32)
                nmr = T(st, name + "nmr", [128, 1], F32)
                b_aT, b_r, b_xres, b_gb, b_stt, b_mv, b_rstd, b_nmr = (Buf() for _ in range(8))
                b_Ws = [Buf() for _ in range(NW)]
                ds_Ws = [fw.new_sem(name + f"dsW{i}") for i in range(NW)]
                ds_a = fw.new_sem(name + "dsa")
                ds_xr = fw.new_sem(name + "dsxr")
                fw.dma(gbc[:], plg[li:li + 1, :].to_broadcast([128, D]), [], [b_gb], ds_c)
                fw.dma(bbc[:], plb[li:li + 1, :].to_broadcast([128, D]), [], [b_gb], ds_c)
                if bias_bc_src is not None:
                    cbc = T(st, name + "cbc", [128, D], F32)
                    fw.dma(cbc[:], bias_bc_src[0:1, :].to_broadcast([128, D]), [], [b_gb], ds_c)
                for tile in ftiles:
                    t0, ntok = tile["t0"], tile["ntok"]
                    nsub = (ntok + 127) // 128
                    nsl = min(128, ntok)
                    fw.dma(aT[:, :, 0:ntok], srcT[tile["fidx"]][:, :, :], [], [b_aT], ds_a)
                    units = [(og, fg) for og in range(8) for fg in range(nfg)]
                    for ui in range(NW - 1):
                        og, fg = units[ui]
                        fw.dma(Ws[ui % NW][:], Wscr[og, fg], [b_Wscr], [b_Ws[ui % NW]], ds_Ws[ui % NW])
                    for ui, (og, fg) in enumerate(units):
                        if ui + NW - 1 < len(units):
                            og2, fg2 = units[ui + NW - 1]
                            s2 = (ui + NW - 1) % NW
                            fw.dma(Ws[s2][:], Wscr[og2, fg2], [b_Wscr], [b_Ws[s2]], ds_Ws[s2])
                        sl = ui % NW
                        if fg == 0:
                            rsrc = resid(tile)[:, og * 512:(og + 1) * 512]
                            if ntok % 128 == 0:
                                fw.dma(xres[:, 0:nsub, :], rsrc.rearrange("(s p) c -> p s c", p=128), [], [b_xres], ds_xr)
                            else:
                                fw.dma(xres[0:nsl, 0, :], rsrc, [], [b_xres], ds_xr)
                        for fl in range(8):
                            fc = fg * 8 + fl
                            for sub in range(nsub):
                                ns = min(128, ntok - sub * 128)
                                bk = sub + 4 * (og % 2)
                                last = (fc == KC - 1)
                                mm(banks[bk][0:ns, :], aT[:, fc, sub * 128:sub * 128 + ns], Ws[sl][:, fl, :], fc == 0, last,
                                   [b_aT, b_Ws[sl]], [bb[bk]], inc=(last or (fl == 7 and sub == nsub - 1)))
                        if fg == nfg - 1:
                            for sub in range(nsub):
                                ns = min(128, ntok - sub * 128)
                                bk = sub + 4 * (og % 2)
                                cs = slice(og * 512, (og + 1) * 512)
                                dve.op(lambda e: e.scalar_tensor_tensor(out=r[0:ns, sub, cs], in0=xres[0:ns, sub, :], scalar=float(ALPHA),
                                                                        in1=banks[bk][0:ns, :], op0=ALU.mult, op1=ALU.add),
                                       [b_xres, bb[bk]], [b_r])
                                if bias_bc_src is not None:
                                    pool.op(lambda e: e.tensor_tensor(out=r[0:ns, sub, cs], in0=r[0:ns, sub, cs], in1=cbc[0:ns, cs], op=ALU.add),
                                            [b_r, b_gb], [b_r])
                    for sub in range(nsub):
                        ns = min(128, ntok - sub * 128)
                        for i in range(8):
                            dve.op(lambda e: e.bn_stats(out=stt[0:ns, i * 6:(i + 1) * 6], in_=r[0:ns, sub, i * 512:(i + 1) * 512]), [b_r], [b_stt])
                        dve.op(lambda e: e.bn_aggr(out=mv[0:ns, :], in_=stt[0:ns, :]), [b_stt], [b_mv])
                        actf(rstd[0:ns, :], mv[0:ns, 1:2], AF.Sqrt, [b_mv], [b_rstd], bias=EPS)
                        dve.op(lambda e: e.reciprocal(out=rstd[0:ns, :], in_=rstd[0:ns, :]), [b_rstd], [b_rstd])
                        dve.op(lambda e: e.scalar_tensor_tensor(out=nmr[0:ns, :], in0=mv[0:ns, 0:1], scalar=-1.0, in1=rstd[0:ns, :],
                                                                op0=ALU.mult, op1=ALU.mult), [b_mv, b_rstd], [b_nmr])
                        actf(r[0:ns, sub, :], r[0:ns, sub, :], AF.Identity, [b_r, b_rstd, b_nmr], [b_r], bias=nmr[0:ns, :], scale=rstd[0:ns, :])
                        pool.op(lambda e: e.tensor_tensor(out=r[0:ns, sub, :], in0=r[0:ns, sub, :], in1=gbc[0:ns, :], op=ALU.mult), [b_r, b_gb], [b_r])
                        dve.op(lambda e: e.tensor_tensor(out=r[0:ns, sub, :], in0=r[0:ns, sub, :], in1=bbc[0:ns, :], op=ALU.add), [b_r, b_gb], [b_r])
                        dst = dst_of_tile(tile, sub, ns)
                        if dst is not None:
                            fw.dma(dst, r[0:ns, sub, :], [b_r], [], store_sem, is_output=True)
                fw.barrier()

        def resid_x(tile):
            return xs if tile["kind"] == "s" else xp[tile["t0"]:tile["t0"] + tile["ntok"], :]

        proj_res_ln("C", GTt, 64, WdB, b_Wd, 8, resid_x, 0, None,
                    lambda tile, sub, ns: X1[tile["t0"] + sub * 128:tile["t0"] + sub * 128 + ns, :], ds_st[2], 2)

        with ExitStack() as st:
            xT = T(st, "DxT", [128, 32, 512], BF16)
            b_xT = Buf()
            c32 = T(st, "c32", [128, 32, 512], F32)
            b_c32 = Buf()
            xin = c32[:].rearrange("p a b -> p (a b)")[:, 0:D]
            szs = [T(st, f"Dszs{i}", [128, 512], BF16) for i in range(2)]
            szl = [T(st, f"Dszl{i}", [128, 512], BF16) for i in range(2)]
            pTs = [T(st, f"DpTs{i}", [128, 512], BF16) for i in range(2)]
            b_szs, b_szl, b_pTs = ([Buf(), Buf()] for _ in range(3))
            b_SZG = [Buf() for _ in range(32)]
            ds_zs = [fw.new_sem(f"ds_zs{i}") for i in range(2)]
            ds_zl = [fw.new_sem(f"ds_zl{i}") for i in range(2)]
            Wsl = [T(st, f"DWsl{i}", [128, 32, 256], BF16) for i in range(2)]
            b_Wsl = [Buf(), Buf()]
            ds_W = [fw.new_sem(f"Dds_W{i}") for i in range(2)]
            ds_x = fw.new_sem("Dds_x")
            Dgs = [T(st, f"Dgs{i}", [128, 31, 128], BF16) for i in range(2)]
            b_Dgs = [Buf(), Buf()]
            ds_Dg = [fw.new_sem(f"ds_Dg{i}") for i in range(2)]
            a32 = [T(st, f"a32{i}", [128, 512], F32) for i in range(2)]
            sg = [T(st, f"sg{i}", [128, 512], F32) for i in range(2)]
            u32 = [T(st, f"u32{i}", [128, 512], F32) for i in range(2)]
            uext = [T(st, f"uext{i}", [128, 544], BF16) for i in range(2)]
            cbb = [T(st, f"cbb{i}", [128, 512], BF16) for i in range(2)]
            csq = [T(st, f"csq{i}", [128, 512], BF16) for i in range(2)]
            b_a32, b_sg, b_u32, b_uext, b_cbb, b_csq = ([Buf(), Buf()] for _ in range(6))
            chPb = T(st, "chPb", [128, 32, 30], BF16)
            chSb = T(st, "chSb", [128, 32, 30 * NSS], BF16)
            bcins = T(st, "bcins", [128, 96], F32)
            bdws = T(st, "bdws", [128, 32], F32)
            clgs = T(st, "clgs", [128, 32], F32)
            clbs = T(st, "clbs", [128, 32], F32)
            mean = T(st, "Dmean", [128, 512], F32)
            rstdD = T(st, "DrstdD", [128, 512], F32)
            mr = T(st, "Dmr", [128, 512], F32)
            t1 = a32
            b_t1 = b_a32
            b_par, b_stat = Buf(), Buf()
            fw.dma(bcins[:], bcin[:, :], [], [b_par], ds_c)
            fw.dma(bdws[:], bdw[:, :], [], [b_par], ds_c)
            fw.dma(clgs[:], clng[:, :], [], [b_par], ds_c)
            fw.dma(clbs[:], clnb[:, :], [], [b_par], ds_c)
            dve.op(lambda e: e.tensor_copy(out=chPb[:], in_=chP32[:]), [b_chP], [b_chP])
            dve.op(lambda e: e.tensor_copy(out=chSb[:], in_=chS32[:]), [b_chS], [b_chS])
            BA = [0, 1]
            BGL = [2, 3]
            BCV, BM, BQ2, BX0, BX1 = 4, 5, 6, 6, 7
            for tile in ftiles:
                t0, ntok, segs = tile["t0"], tile["ntok"], tile["segs"]
                is_s = tile["kind"] == "s"
                ch32, chb, b_ch = (chS32, chSb, b_chS) if is_s else (chP32, chPb, b_chP)
                load_xT((xT, b_xT, xin, b_c32, ds_x, (BX0, BX1)), tile, X1[t0:t0 + ntok, :])
                order = []
                for i in range(16):
                    order += [i, 16 + i]
                order += list(range(32, 48))
                fw.dma(Wsl[0][:], WcinB[order[0]], [b_Wcin], [b_Wsl[0]], ds_W[0])

                def stage_U(k, u, cc):
                    if cc == 0 and k + 1 < len(order):
                        s2 = (k + 1) % 2
                        fw.dma(Wsl[s2][:], WcinB[order[k + 1]], [b_Wcin], [b_Wsl[s2]], ds_W[s2])
                    sl = k % 2
                    oc = 2 * u + cc
                    kind = oc // 32
                    c = oc % 32
                    par = c % 2
                    bk = (BA if kind != 1 else BGL)[par]
                    if kind == 0:
                        fw.dma(Dgs[par][:], DgD[c], [b_Dg], [b_Dgs[par]], ds_Dg[par])
                    for kc in range(32):
                        mm(banks[bk][:, 0:ntok], Wsl[sl][:, kc, cc * 128:(cc + 1) * 128], xT[:, kc, 0:ntok],
                           kc == 0, kc == 31, [b_Wsl[sl], b_xT], [bb[bk]], inc=(kc == 31))
                    if kind == 0:
                        actf(a32[par][:, 0:ntok], banks[bk][:, 0:ntok], AF.Identity, [bb[bk], b_par], [b_a32[par]], bias=bcins[:, oc:oc + 1])
                    elif kind == 2:
                        actf(szs[par][:, 0:ntok], banks[bk][:, 0:ntok], AF.Silu, [bb[bk], b_par], [b_szs[par]], bias=bcins[:, oc:oc + 1])
                        fw.dma(SZG[c * 128:(c + 1) * 128, t0:t0 + ntok], szs[par][:, 0:ntok], [b_szs[par]], [b_SZG[c]], ds_zs[par])
                    else:
                        actf(sg[par][:, 0:ntok], banks[bk][:, 0:ntok], AF.Sigmoid, [bb[bk], b_par], [b_sg[par]], bias=bcins[:, oc:oc + 1])
                        dve.op(lambda e: e.tensor_tensor(out=u32[par][:, 0:ntok], in0=a32[par][:, 0:ntok], in1=sg[par][:, 0:ntok], op=ALU.mult),
                               [b_a32[par], b_sg[par]], [b_u32[par]])
                        for si, (so, sln) in enumerate(segs):
                            eo = si * (sln + 30)
                            pool.op(lambda e: e.tensor_copy(out=uext[par][:, eo:eo + 30], in_=chb[:, c, si * 30:(si + 1) * 30]), [b_ch], [b_uext[par]])
                            pool.op(lambda e: e.tensor_copy(out=uext[par][:, eo + 30:eo + 30 + sln], in_=u32[par][:, so:so + sln]),
                                    [b_u32[par]], [b_uext[par]])
                            if tile["eblk"]:
                                nh = 30 + (OWN0 - t0)
                                pool.op(lambda e: e.tensor_scalar(out=uext[par][:, eo:eo + nh], in0=uext[par][:, eo:eo + nh], scalar1=flag[:, 0:1],
                                                                  scalar2=None, op0=ALU.mult), [b_uext[par], b_const], [b_uext[par]])
                            dve.op(lambda e: e.tensor_copy(out=ch32[:, c, si * 30:(si + 1) * 30], in_=u32[par][:, so + sln - 30:so + sln]),
                                   [b_u32[par]], [b_ch])
                            pool.op(lambda e: e.tensor_copy(out=chb[:, c, si * 30:(si + 1) * 30], in_=uext[par][:, eo + sln:eo + sln + 30]),
                                    [b_uext[par]], [b_ch])
                        return c
                    return None

                def stage_conv(c):
                    par = c % 2
                    for si, (so, sln) in enumerate(segs):
                        eo = si * (sln + 30)
                        for j in range(31):
                            mm(banks[BCV][:, so:so + sln], Dgs[par][:, j, :], uext[par][:, eo + j:eo + j + sln], j == 0, j == 30,
                               [b_Dgs[par], b_uext[par]], [bb[BCV]], inc=(j == 30))
                    actf(c32[:, c, 0:ntok], banks[BCV][:, 0:ntok], AF.Identity, [bb[BCV], b_par], [b_c32], bias=bdws[:, c:c + 1])
                    pool.op(lambda e: e.tensor_copy(out=cbb[par][:, 0:ntok], in_=c32[:, c, 0:ntok]), [b_c32], [b_cbb[par]])
                    actf(csq[par][:, 0:ntok], c32[:, c, 0:ntok], AF.Square, [b_c32], [b_csq[par]])

                def stage_stats(c):
                    par = c % 2
                    mm(banks[BM][:, 0:ntok], onesb[:, :], cbb[par][:, 0:ntok], c == 0, c == 31, [b_const, b_cbb[par]], [bb[BM]], inc=True)
                    mm(banks[BQ2][:, 0:ntok], onesb[:, :], csq[par][:, 0:ntok], c == 0, c == 31, [b_const, b_csq[par]], [bb[BQ2]], inc=True)
                    if c == 31:
                        actf(mean[:, 0:ntok], banks[BM][:, 0:ntok], AF.Identity, [bb[BM]], [b_stat], scale=1.0 / D)
                        dve.op(lambda e: e.tensor_tensor(out=mr[:, 0:ntok], in0=mean[:, 0:ntok], in1=mean[:, 0:ntok], op=ALU.mult), [b_stat], [b_stat])
                        dve.op(lambda e: e.scalar_tensor_tensor(out=rstdD[:, 0:ntok], in0=banks[BQ2][:, 0:ntok], scalar=1.0 / D, in1=mr[:, 0:ntok],
                                                                op0=ALU.mult, op1=ALU.subtract), [bb[BQ2], b_stat], [b_stat])
                        dve.op(lambda e: e.tensor_scalar(out=rstdD[:, 0:ntok], in0=rstdD[:, 0:ntok], scalar1=0.0, scalar2=None, op0=ALU.max),
                               [b_stat], [b_stat])
                        actf(rstdD[:, 0:ntok], rstdD[:, 0:ntok], AF.Sqrt, [b_stat], [b_stat], bias=EPS)
                        dve.op(lambda e: e.reciprocal(out=rstdD[:, 0:ntok], in_=rstdD[:, 0:ntok]), [b_stat], [b_stat])
                        dve.op(lambda e: e.tensor_tensor(out=mr[:, 0:ntok], in0=mean[:, 0:ntok], in1=rstdD[:, 0:ntok], op=ALU.mult), [b_stat], [b_stat])

                pend_conv, pend_stats = [], []
                for k, u in enumerate(order):
                    for cc in range(2):
                        newc = stage_U(k, u, cc)
                        do_stats, pend_stats = pend_stats, []
                        do_conv, pend_conv = pend_conv, []
                        for c in do_conv:
                            stage_conv(c)
                            pend_stats.append(c)
                        for c in do_stats:
                            stage_stats(c)
                        if newc is not None:
                            pend_conv.append(newc)
                assert not pend_conv and not pend_stats
                for c in range(32):
                    par = c % 2
                    dve.op(lambda e: e.tensor_tensor(out=t1[par][:, 0:ntok], in0=c32[:, c, 0:ntok], in1=rstdD[:, 0:ntok], op=ALU.mult),
                           [b_c32, b_stat], [b_t1[par]])
                    dve.op(lambda e: e.tensor_tensor(out=t1[par][:, 0:ntok], in0=t1[par][:, 0:ntok], in1=mr[:, 0:ntok], op=ALU.subtract),
                           [b_t1[par], b_stat], [b_t1[par]])
                    actf(t1[par][:, 0:ntok], t1[par][:, 0:ntok], AF.Silu, [b_t1[par], b_par], [b_t1[par]], bias=clbs[:, c:c + 1], scale=clgs[:, c:c + 1])
                    fw.dma(szl[par][:, 0:ntok], SZG[c * 128:(c + 1) * 128, t0:t0 + ntok], [b_SZG[c]], [b_szl[par]], ds_zl[par])
                    pool.op(lambda e: e.tensor_tensor(out=pTs[par][:, 0:ntok], in0=szl[par][:, 0:ntok], in1=t1[par][:, 0:ntok], op=ALU.mult),
                            [b_szl[par], b_t1[par]], [b_pTs[par]])
                    fw.dma(PTt[tile["fidx"]][:, c, :], pTs[par][:, 0:ntok], [b_pTs[par]], [], ds_st[3])
            fw.barrier()

        with ExitStack() as st:
            stg = T(st, "stgc", [30, D], F32)
            b_stg = Buf()
            for (hsrc, bh, off, dst) in [(chP32, b_chP, 0, pcc)] + [(chS32, b_chS, 30 * i, occ[i]) for i in range(NSS)]:
                for c in range(32):
                    bk = c % 2
                    tr(banks[bk][0:30, 0:128], hsrc[:, c, off:off + 30], ident[:, :], [bh, b_const], [bb[bk]])
                    act.op(lambda e: e.copy(out=stg[:, c * 128:(c + 1) * 128], in_=banks[bk][0:30, 0:128]), [bb[bk]], [b_stg])
                fw.dma(dst, stg[:, :], [b_stg], [], ds_out, is_output=True)
            fw.barrier()

        def resid_x1(tile):
            return X1[tile["t0"]:tile["t0"] + tile["ntok"], :]

        def dst_y(tile, sub, ns):
            if tile["kind"] == "s":
                return ys[sub * 128:sub * 128 + ns, :]
            tok = tile["t0"] + sub * 128
            if tok < OWN0:
                return None
            return yp[tok - OWN0:tok - OWN0 + ns, :]

        proj_res_ln("E", PTt, 32, WcoB, b_Wco, 4, resid_x1, 1, bcout, dst_y, ds_out, 3)
        fw.finish()
    return nc


def _colp(v, n):
    return np.ascontiguousarray(np.asarray(v, np.float32).reshape(n, 128).T)


def kernel(x_prompt, x_sample, state_mlstm_C, state_mlstm_n, state_mlstm_m, state_mlstm_conv, state_conformer_conv,
           w_up, w_mconv, b_mconv, w_q, w_k, w_v, w_gate, b_gate, mh_gain, skip, w_down,
           w_cin, b_cin, w_dw, b_dw, cln_g, cln_b, w_cout, b_cout, post_ln_g, post_ln_b):
    f = np.float32
    A = lambda a: np.ascontiguousarray(np.asarray(a, f))
    shared = {}
    shared["w_up"] = A(w_up[0])
    shared["w_down"] = A(w_down[0])
    shared["w_cin"] = A(w_cin[0])
    shared["w_cout"] = A(w_cout[0])
    shared["wmc"] = A(np.asarray(w_mconv[0], f).reshape(4, 64, 128).transpose(2, 1, 0))
    shared["bmc"] = _colp(b_mconv[0], 64)
    shared["skipc"] = _colp(skip[0], 64)
    shared["gainc"] = _colp(mh_gain[0], 64)
    for nm, w in (("wq4", w_q), ("wk4", w_k), ("wv4", w_v)):
        shared[nm] = A(np.asarray(w[0], f).reshape(64, 128, 4).transpose(1, 0, 2))
    wg = np.asarray(w_gate[0], f).reshape(3, 64, 128, 16).transpose(2, 0, 1, 3)
    wg40 = np.zeros((128, 3, 64, 40), f)
    wg40[..., 0:8] = wg[..., 0:8]
    wg40[..., 32:40] = wg[..., 8:16]
    shared["wg40"] = wg40.reshape(128, 3 * 64 * 40)
    bg40 = np.zeros((40, 1), f)
    bg40[0:8, 0] = np.asarray(b_gate[0], f)[0:8]
    bg40[32:40, 0] = np.asarray(b_gate[0], f)[8:16]
    shared["bg40"] = bg40
    shared["bcin"] = _colp(b_cin[0], 96)
    shared["wdw"] = A(np.asarray(w_dw[0], f).reshape(31, 32, 128).transpose(2, 1, 0)).reshape(128, 32 * 31)
    shared["bdw"] = _colp(b_dw[0], 32)
    shared["clng"] = _colp(cln_g[0], 32)
    shared["clnb"] = _colp(cln_b[0], 32)
    shared["bcout"] = A(np.asarray(b_cout[0], f).reshape(1, D))
    shared["plg"] = A(post_ln_g)
    shared["plb"] = A(post_ln_b)
    shared["ident"] = np.eye(128, dtype=f)
    shared["mask01"] = np.triu(np.ones((128, 128), f))
    pidx = np.arange(128)[:, None] // 4
    shared["bdmask"] = (pidx == np.arange(32)[None, :]).astype(f)

    xp_all = np.asarray(x_prompt, f)
    xs_all = np.asarray(x_sample, f)
    in_maps = []
    for c in range(NCORES):
        m = dict(shared)
        if c % 2 == 0:
            xv = np.zeros((TP, D), f)
            xv[OWN0:] = xp_all[c // 2, 0:TP - OWN0]
            m["xp"] = xv
            m["flag"] = np.zeros((128, 1), f)
        else:
            m["xp"] = A(xp_all[c // 2])
            m["flag"] = np.ones((128, 1), f)
        m["xs"] = A(xs_all[2 * c:2 * c + 2].reshape(TS, D))
        m["sC"] = A(state_mlstm_C[0, 2 * c:2 * c + 2])
        m["sn"] = A(state_mlstm_n[0, 2 * c:2 * c + 2])
        m["sm"] = A(state_mlstm_m[0, 2 * c:2 * c + 2])
        m["smc"] = A(state_mlstm_conv[0, 2 * c:2 * c + 2])
        m["scc"] = A(state_conformer_conv[0, 2 * c:2 * c + 2])
        in_maps.append(m)

    nc = build_program()
    res = run_bass_kernel_spmd(nc, in_maps, core_ids=list(range(NCORES)))
    R = res.results
    B = xp_all.shape[0]
    y_prompt = np.stack([np.concatenate([R[2 * b]["yp"], R[2 * b + 1]["yp"]], axis=0) for b in range(B)]).astype(f)
    y_sample = np.concatenate([R[c]["ys"].reshape(NSS, SL, D) for c in range(NCORES)], axis=0).astype(f)
    p_C = np.stack([R[2 * b + 1]["pC"] for b in range(B)])[None].astype(f)
    p_n = np.stack([R[2 * b + 1]["pn"] for b in range(B)])[None].astype(f)
    p_m = np.stack([R[2 * b + 1]["pm"].reshape(NH) for b in range(B)])[None].astype(f)
    p_mc = np.stack([R[2 * b + 1]["pmc"] for b in range(B)])[None].astype(f)
    p_cc = np.stack([R[2 * b + 1]["pcc"] for b in range(B)])[None].astype(f)
    s_C = np.concatenate([R[c]["oC"] for c in range(NCORES)], axis=0)[None].astype(f)
    s_n = np.concatenate([R[c]["on"] for c in range(NCORES)], axis=0)[None].astype(f)
    s_m = np.concatenate([R[c]["om"].reshape(NSS, NH) for c in range(NCORES)], axis=0)[None].astype(f)
    s_mc = np.concatenate([R[c]["omc"] for c in range(NCORES)], axis=0)[None].astype(f)
    s_cc = np.concatenate([R[c]["occ"] for c in range(NCORES)], axis=0)[None].astype(f)
    return (y_prompt, y_sample, p_C, p_n, p_m, p_mc, p_cc, s_C, s_n, s_m, s_mc, s_cc)
```
